# Optimizing a Trainium2 kernel written in Bass

```python
import math
import jax, jax.numpy as jnp
from jax import lax
import numpy as np

D_MODEL = 4096
BATCH = 4
SEQ = 4096
DEPTH = 2

GRID_W = 64
CTX_LEN = 256
N_MIXERS = 2
EPS = 1e-6

DA_HEADS = 16
DA_HEAD_DIM = 128
DA_V_DIM = 2 * DA_HEAD_DIM
DA_Q_W = DA_HEADS * 2 * DA_HEAD_DIM
DA_K_W = DA_HEADS * 2 * DA_HEAD_DIM
DA_V_W = DA_HEADS * DA_V_DIM
DA_GATE_W = DA_V_W
DA_IN_W = DA_Q_W + DA_K_W + DA_V_W + DA_GATE_W
DA_K0 = DA_Q_W
DA_V0 = DA_Q_W + DA_K_W
DA_G0 = DA_Q_W + DA_K_W + DA_V_W
Q_BLOCK = 128
ROPE_BASE = 10000.0

S5_WIDTH = D_MODEL
S5_GROUP = 16
S5_GROUPS = S5_WIDTH // S5_GROUP
S5_STATE = 64
DT_MIN = 1e-3
DT_MAX = 1e-1

N_ATTN_LAYERS = (DEPTH + 1) // 2
N_S5_LAYERS = DEPTH // 2

kernel_name = 'hybrid_diffattn_s5_prefix_dit'


def rmsnorm(x, g):
    xf = x.astype(jnp.float32)
    y = xf * lax.rsqrt(jnp.mean(xf * xf, axis=-1, keepdims=True) + EPS)
    return (y * g.astype(jnp.float32)).astype(x.dtype)


def axial_rope_tables(n_tokens):
    rows = n_tokens // GRID_W
    row = jnp.repeat(jnp.arange(rows), GRID_W)
    col = jnp.tile(jnp.arange(GRID_W), rows)
    n_freq = DA_HEAD_DIM // 4
    inv_freq = ROPE_BASE ** (-jnp.arange(n_freq, dtype=jnp.float32) / n_freq)
    ang = jnp.stack([row, col], axis=-1).astype(jnp.float32)[:, :, None] * inv_freq
    return jnp.cos(ang)[:, :, None, :], jnp.sin(ang)[:, :, None, :]


def apply_axial_rope(t, cos, sin):
    shp = t.shape
    tr = t.reshape(shp[:-1] + (2, 2, shp[-1] // 4))
    rot = jnp.stack([-tr[..., 1, :], tr[..., 0, :]], axis=-2)
    return (tr * cos.astype(t.dtype) + rot * sin.astype(t.dtype)).reshape(shp)


def split_qk(t):
    b, n = t.shape[:2]
    return t.reshape(b, n, DA_HEADS, 2, DA_HEAD_DIM).transpose(3, 0, 2, 1, 4)


def split_v(t):
    b, n = t.shape[:2]
    return t.reshape(b, n, DA_HEADS, DA_V_DIM).transpose(0, 2, 1, 3)


def diff_softmax_core(q, k, v, lam):
    s = jnp.einsum('nbhqd,nbhkd->nbhqk', q, k).astype(jnp.float32) * (DA_HEAD_DIM ** -0.5)
    p = jax.nn.softmax(s, axis=-1)
    w = p[0] - lam * p[1]
    return jnp.einsum('bhqk,bhkv->bhqv', w.astype(v.dtype), v)


def diff_attn_mixer(h_lat, h_ctx, w_in, w_out, lam_vecs, subln_g, lam_init, need_ctx_out):
    bsz, n_lat, _ = h_lat.shape
    p_lat = h_lat @ w_in
    cos, sin = axial_rope_tables(n_lat)
    q_lat = apply_axial_rope(split_qk(p_lat[..., :DA_K0]), cos, sin)
    k_lat = apply_axial_rope(split_qk(p_lat[..., DA_K0:DA_V0]), cos, sin)
    v_lat = split_v(p_lat[..., DA_V0:DA_G0])
    g_lat = p_lat[..., DA_G0:]
    if need_ctx_out:
        p_ctx = h_ctx @ w_in
        kv_ctx = p_ctx[..., DA_K0:DA_G0]
    else:
        kv_ctx = h_ctx @ w_in[:, DA_K0:DA_G0]
    k_ctx = split_qk(kv_ctx[..., :DA_K_W])
    v_ctx = split_v(kv_ctx[..., DA_K_W:])
    lv = lam_vecs.astype(jnp.float32)
    lam = jnp.exp(jnp.sum(lv[0] * lv[1])) - jnp.exp(jnp.sum(lv[2] * lv[3])) + lam_init
    k_all = jnp.concatenate([k_ctx, k_lat], axis=3)
    v_all = jnp.concatenate([v_ctx, v_lat], axis=2)
    n_blk = n_lat // Q_BLOCK
    q_blocks = q_lat.reshape(2, bsz, DA_HEADS, n_blk, Q_BLOCK, DA_HEAD_DIM).transpose(3, 0, 1, 2, 4, 5)
    o_blocks = lax.map(lambda qb: diff_softmax_core(qb, k_all, v_all, lam), q_blocks)
    o_lat = o_blocks.transpose(1, 2, 0, 3, 4).reshape(bsz, DA_HEADS, n_lat, DA_V_DIM)

    def finish(o, gate):
        o = rmsnorm(o, subln_g) * (1.0 - lam_init)
        o = o.transpose(0, 2, 1, 3).reshape(o.shape[0], o.shape[2], DA_V_W)
        return (o * jax.nn.silu(gate)) @ w_out

    out_lat = finish(o_lat, g_lat)
    out_ctx = None
    if need_ctx_out:
        q_ctx = split_qk(p_ctx[..., :DA_K0])
        o_ctx = diff_softmax_core(q_ctx, k_ctx, v_ctx, lam)
        out_ctx = finish(o_ctx, p_ctx[..., DA_G0:])
    return out_lat, out_ctx


def s5_discretize(A_re, A_im, log_dt, B_re, B_im):
    A_re = A_re.astype(jnp.float32)
    A_im = A_im.astype(jnp.float32)
    dt = jnp.exp(log_dt.astype(jnp.float32))[:, None]
    mag = jnp.exp(A_re * dt)
    a_re = mag * jnp.cos(A_im * dt)
    a_im = mag * jnp.sin(A_im * dt)
    den = A_re * A_re + A_im * A_im
    f_re = ((a_re - 1.0) * A_re + a_im * A_im) / den
    f_im = (a_im * A_re - (a_re - 1.0) * A_im) / den
    B_re = B_re.astype(jnp.float32)
    B_im = B_im.astype(jnp.float32)
    bb_re = f_re[..., None] * B_re - f_im[..., None] * B_im
    bb_im = f_re[..., None] * B_im + f_im[..., None] * B_re
    return a_re, a_im, bb_re, bb_im


def ssm_combine(e1, e2):
    a1r, a1i, b1r, b1i = e1
    a2r, a2i, b2r, b2i = e2
    return (a2r * a1r - a2i * a1i,
            a2r * a1i + a2i * a1r,
            a2r * b1r - a2i * b1i + b2r,
            a2r * b1i + a2i * b1r + b2i)


def s5_scan(u, a_re, a_im, bb_re, bb_im, h0):
    bu_re = jnp.einsum('lbgc,gpc->lbgp', u, bb_re)
    bu_im = jnp.einsum('lbgc,gpc->lbgp', u, bb_im)
    if h0 is not None:
        h0_re, h0_im = h0
        bu_re = bu_re.at[0].add(a_re * h0_re - a_im * h0_im)
        bu_im = bu_im.at[0].add(a_re * h0_im + a_im * h0_re)
    n = u.shape[0]
    ar = jnp.broadcast_to(a_re, (n, 1) + a_re.shape)
    ai = jnp.broadcast_to(a_im, (n, 1) + a_im.shape)
    _, _, h_re, h_im = lax.associative_scan(ssm_combine, (ar, ai, bu_re, bu_im), axis=0)
    return h_re, h_im


def s5_readout(h_re, h_im, C_re, C_im):
    return (jnp.einsum('lbgp,gcp->lbgc', h_re, C_re.astype(jnp.float32))
            - jnp.einsum('lbgp,gcp->lbgc', h_im, C_im.astype(jnp.float32)))


def to_groups(u):
    b, n = u.shape[:2]
    return u.astype(jnp.float32).reshape(b, n, S5_GROUPS, S5_GROUP).transpose(1, 0, 2, 3)


def from_groups(y):
    n, b = y.shape[:2]
    return y.transpose(1, 0, 2, 3).reshape(b, n, S5_WIDTH)


def s5_mixer(h_lat, h_ctx, w_in, A_re, A_im, log_dt, B_re, B_im, C_re, C_im, d_skip, w_glu, w_out, need_ctx_out):
    p_lat = h_lat @ w_in
    u_lat, z_lat = p_lat[..., :S5_WIDTH], p_lat[..., S5_WIDTH:]
    if need_ctx_out:
        p_ctx = h_ctx @ w_in
        u_ctx, z_ctx = p_ctx[..., :S5_WIDTH], p_ctx[..., S5_WIDTH:]
    else:
        u_ctx = h_ctx @ w_in[:, :S5_WIDTH]
    ug_lat = to_groups(u_lat)
    ug_ctx = to_groups(u_ctx)
    y_lat = jnp.zeros_like(ug_lat)
    y_ctx = jnp.zeros_like(ug_ctx)
    for d in range(2):
        a_re, a_im, bb_re, bb_im = s5_discretize(A_re[d], A_im[d], log_dt[d], B_re[d], B_im[d])
        uc = ug_ctx if d == 0 else ug_ctx[::-1]
        ul = ug_lat if d == 0 else ug_lat[::-1]
        hc_re, hc_im = s5_scan(uc, a_re, a_im, bb_re, bb_im, None)
        hl_re, hl_im = s5_scan(ul, a_re, a_im, bb_re, bb_im, (hc_re[-1], hc_im[-1]))
        yl = s5_readout(hl_re, hl_im, C_re[d], C_im[d])
        y_lat = y_lat + (yl if d == 0 else yl[::-1])
        if need_ctx_out:
            yc = s5_readout(hc_re, hc_im, C_re[d], C_im[d])
            y_ctx = y_ctx + (yc if d == 0 else yc[::-1])
    dk = d_skip.astype(jnp.float32)

    def finish(y, u, z):
        y = (from_groups(y) + dk * u.astype(jnp.float32)).astype(u.dtype)
        y = jax.nn.gelu(y)
        y = y * jax.nn.sigmoid(y @ w_glu)
        return (y * jax.nn.silu(z)) @ w_out

    out_lat = finish(y_lat, u_lat, z_lat)
    out_ctx = finish(y_ctx, u_ctx, z_ctx) if need_ctx_out else None
    return out_lat, out_ctx


def setup_inputs(seed: int = 0) -> dict:
    key = jax.random.key(seed)
    ks = jax.random.split(key, 24)
    nrm = jax.random.normal
    E, G, P, C16 = S5_WIDTH, S5_GROUPS, S5_STATE, S5_GROUP
    NA, NS = N_ATTN_LAYERS, N_S5_LAYERS
    x = nrm(ks[0], (BATCH, SEQ, D_MODEL), jnp.float32)
    c = nrm(ks[1], (BATCH, D_MODEL), jnp.float32)
    ctx = nrm(ks[2], (BATCH, CTX_LEN, D_MODEL), jnp.float32)
    c_ctx = nrm(ks[3], (D_MODEL,), jnp.float32)
    ada_w = nrm(ks[4], (DEPTH, D_MODEL, 3 * D_MODEL), jnp.float32) * D_MODEL ** -0.5
    ada_b = 0.01 * nrm(ks[5], (DEPTH, 3 * D_MODEL), jnp.float32)
    norm_pre = 1.0 + 0.02 * nrm(ks[6], (DEPTH, D_MODEL), jnp.float32)
    norm_post = 1.0 + 0.02 * nrm(ks[7], (DEPTH, D_MODEL), jnp.float32)
    attn_w_in = nrm(ks[8], (NA, D_MODEL, DA_IN_W), jnp.float32) * D_MODEL ** -0.5
    attn_w_out = nrm(ks[9], (NA, DA_V_W, D_MODEL), jnp.float32) * DA_V_W ** -0.5
    attn_lam = 0.1 * nrm(ks[10], (NA, 4, DA_HEAD_DIM), jnp.float32)
    attn_subln = 1.0 + 0.02 * nrm(ks[11], (NA, DA_V_DIM), jnp.float32)
    s5_w_in = nrm(ks[12], (NS, D_MODEL, 2 * E), jnp.float32) * D_MODEL ** -0.5
    s5_A_re = -0.5 + 0.01 * nrm(ks[13], (NS, 2, G, P), jnp.float32)
    s5_A_im = math.pi * jnp.arange(P, dtype=jnp.float32) + 0.01 * nrm(ks[14], (NS, 2, G, P), jnp.float32)
    s5_log_dt = jax.random.uniform(ks[15], (NS, 2, G), jnp.float32, math.log(DT_MIN), math.log(DT_MAX))
    s5_B_re = nrm(ks[16], (NS, 2, G, P, C16), jnp.float32) * (2 * C16) ** -0.5
    s5_B_im = nrm(ks[17], (NS, 2, G, P, C16), jnp.float32) * (2 * C16) ** -0.5
    s5_C_re = nrm(ks[18], (NS, 2, G, C16, P), jnp.float32) * (2 * P) ** -0.5
    s5_C_im = nrm(ks[19], (NS, 2, G, C16, P), jnp.float32) * (2 * P) ** -0.5
    s5_D = 0.5 * nrm(ks[20], (NS, E), jnp.float32)
    s5_w_glu = nrm(ks[21], (NS, E, E), jnp.float32) * E ** -0.5
    s5_w_out = nrm(ks[22], (NS, E, D_MODEL), jnp.float32) * E ** -0.5
    return {'x': x, 'c': c, 'ctx': ctx, 'c_ctx': c_ctx,
            'ada_w': ada_w, 'ada_b': ada_b, 'norm_pre': norm_pre, 'norm_post': norm_post,
            'attn_w_in': attn_w_in, 'attn_w_out': attn_w_out, 'attn_lam': attn_lam, 'attn_subln': attn_subln,
            's5_w_in': s5_w_in, 's5_A_re': s5_A_re, 's5_A_im': s5_A_im, 's5_log_dt': s5_log_dt,
            's5_B_re': s5_B_re, 's5_B_im': s5_B_im, 's5_C_re': s5_C_re, 's5_C_im': s5_C_im,
            's5_D': s5_D, 's5_w_glu': s5_w_glu, 's5_w_out': s5_w_out}


def reference(x, c, ctx, c_ctx, ada_w, ada_b, norm_pre, norm_post,
              attn_w_in, attn_w_out, attn_lam, attn_subln,
              s5_w_in, s5_A_re, s5_A_im, s5_log_dt, s5_B_re, s5_B_im, s5_C_re, s5_C_im,
              s5_D, s5_w_glu, s5_w_out):
    x_lat, x_ctx = x, ctx
    for i in range(DEPTH):
        need_ctx_out = i < DEPTH - 1
        mod = jax.nn.silu(c) @ ada_w[i] + ada_b[i]
        mod_c = jax.nn.silu(c_ctx) @ ada_w[i] + ada_b[i]
        shift, scale, gate = jnp.split(mod, 3, axis=-1)
        shift_c, scale_c, gate_c = jnp.split(mod_c, 3, axis=-1)
        h_lat = rmsnorm(x_lat, norm_pre[i]) * (1.0 + scale[:, None, :]) + shift[:, None, :]
        h_ctx = rmsnorm(x_ctx, norm_pre[i]) * (1.0 + scale_c) + shift_c
        j = i // N_MIXERS
        if i % N_MIXERS == 0:
            lam_init = 0.8 - 0.6 * math.exp(-0.3 * i)
            out_lat, out_ctx = diff_attn_mixer(h_lat, h_ctx, attn_w_in[j], attn_w_out[j], attn_lam[j],
                                               attn_subln[j], lam_init, need_ctx_out)
        else:
            out_lat, out_ctx = s5_mixer(h_lat, h_ctx, s5_w_in[j], s5_A_re[j], s5_A_im[j], s5_log_dt[j],
                                        s5_B_re[j], s5_B_im[j], s5_C_re[j], s5_C_im[j], s5_D[j],
                                        s5_w_glu[j], s5_w_out[j], need_ctx_out)
        x_lat = x_lat + gate[:, None, :] * rmsnorm(out_lat, norm_post[i])
        if need_ctx_out:
            x_ctx = x_ctx + gate_c * rmsnorm(out_ctx, norm_post[i])
    return x_lat
```

```python
import numpy as np
import concourse.bass as bass
import concourse.mybir as mybir

F32 = mybir.dt.float32
BF16 = mybir.dt.bfloat16
I32 = mybir.dt.int32
ALU = mybir.AluOpType
AF = mybir.ActivationFunctionType
AX = mybir.AxisListType

ENGINES = ("pe", "act", "dve", "pool", "sp")
NDMASEM = {"sp": 8, "pool": 4, "act": 2}


class _I:
    __slots__ = ("eng", "fn", "deps", "dma", "idx", "needed", "sem", "val", "qidx")

    def __init__(self, eng, fn, dma):
        self.eng = eng
        self.fn = fn
        self.dma = dma
        self.deps = []
        self.needed = False
        self.sem = None
        self.val = 0


class Em:
    def __init__(self, nc, same_engine_sync=True):
        self.nc = nc
        self.streams = {e: [] for e in ENGINES}
        self.lastw = {}
        self.readers = {}
        self.same = same_engine_sync
        self.ndma = {e: 0 for e in ENGINES}
        self.dma_hist = {e: [] for e in ENGINES}
        self._bar = None

    def barrier(self):
        lasts = []
        for e in ENGINES:
            st = self.streams[e]
            if st:
                lasts.append(st[-1])
            if e in NDMASEM:
                lasts += self.dma_hist[e][-NDMASEM[e]:]
        self._bar = (lasts, set())

    def _add(self, eng, fn, reads, writes, dma=False):
        ins = _I(eng, fn, dma)
        deps = []
        if self._bar is not None and eng not in self._bar[1]:
            deps += self._bar[0]
            self._bar[1].add(eng)
        for k in reads:
            w = self.lastw.get(k)
            if w is not None:
                deps.append(w)
            if isinstance(k, str) and k.startswith("P:"):
                for r in self.readers.get(k, ()):
                    if r.eng != eng:
                        deps.append(r)
        for k in writes:
            w = self.lastw.get(k)
            if w is not None:
                deps.append(w)
            for r in self.readers.get(k, ()):
                deps.append(r)
        if dma:
            n = self.ndma[eng]
            ins.qidx = n
            ns = NDMASEM[eng]
            hist = self.dma_hist[eng]
            if n >= ns:
                deps.append(hist[n - ns])
            hist.append(ins)
            self.ndma[eng] = n + 1
        seen = set()
        for d in deps:
            if d is ins or id(d) in seen:
                continue
            seen.add(id(d))
            if (not d.dma) and d.eng == eng:
                if eng == "pe" or not self.same:
                    continue
            ins.deps.append(d)
            d.needed = True
        for k in reads:
            self.readers.setdefault(k, []).append(ins)
        for k in writes:
            self.lastw[k] = ins
            self.readers[k] = []
        self.streams[eng].append(ins)
        return ins

    def op(self, eng, fn, reads=(), writes=()):
        return self._add(eng, fn, reads, writes, False)

    def dma(self, eng, out, in_, reads=(), writes=(), **kw):
        return self._add(eng, lambda e: e.dma_start(out=out, in_=in_, **kw), reads, writes, True)

    def finish(self, final_waits=()):
        nc = self.nc
        ctx = []
        sems = {}
        for e in ("pe", "act", "dve", "pool"):
            g = nc.semaphore("s_" + e)
            sems[e] = g.__enter__()
            ctx.append(g)
        dsems = {}
        for e, n in NDMASEM.items():
            lst = []
            for i in range(n):
                g = nc.semaphore("d_%s%d" % (e, i))
                lst.append(g.__enter__())
                ctx.append(g)
            dsems[e] = lst
        for e in ENGINES:
            c = 0
            dc = {}
            for ins in self.streams[e]:
                if ins.dma:
                    s = ins.qidx % NDMASEM[e]
                    dc[s] = dc.get(s, 0) + 16
                    ins.sem = dsems[e][s]
                    ins.val = dc[s]
                elif ins.needed:
                    c += 1
                    ins.sem = sems[e]
                    ins.val = c
        for w in final_waits:
            w.needed = True
        handles = {"pe": "tensor", "act": "scalar", "dve": "vector", "pool": "gpsimd", "sp": "sync"}
        streams = self.streams
        with nc.Block() as block:
            def mk(e):
                def body(eng):
                    have = {}
                    for ins in streams[e]:
                        for d in ins.deps:
                            key = id(d.sem)
                            if have.get(key, 0) >= d.val:
                                continue
                            have[key] = d.val
                            eng.wait_ge(d.sem, d.val)
                        r = ins.fn(eng)
                        if ins.dma:
                            r.then_inc(ins.sem, 16)
                        elif ins.needed:
                            r.then_inc(ins.sem, 1)
                    if e == "sp":
                        for w in final_waits:
                            eng.wait_ge(w.sem, w.val)
                return body
            for e in ENGINES:
                if streams[e] or e == "sp":
                    getattr(block, handles[e])(mk(e))
        for g in reversed(ctx):
            g.__exit__(None, None, None)

from concourse.bass_utils import run_bass_kernel_spmd
import ml_dtypes

NBF = ml_dtypes.bfloat16
NCORE = 8
D = 4096
NJ = 32
NCTX = 256
NLAT = 2048
NT = NCTX + NLAT
BW = 256
EPS = 1e-6
PI = float(np.pi)


class KB:
    def __init__(self):
        self.nc = bass.Bass("TRN2", target_bir_lowering=False)
        self.em = Em(self.nc)
        self.cm = []
        self.outs = []

    def sb(self, name, shape, dt=F32):
        g = self.nc.sbuf_tensor("s_" + name, list(shape), dt)
        t = g.__enter__()
        self.cm.append(g)
        return t

    def ps(self, name, shape, dt=F32):
        g = self.nc.psum_tensor("p_" + name, list(shape), dt)
        t = g.__enter__()
        self.cm.append(g)
        return t

    def mark(self):
        return len(self.cm)

    def release(self, m):
        self.em.barrier()
        while len(self.cm) > m:
            self.cm.pop().__exit__(None, None, None)

    def din(self, name, shape, dt=F32):
        return self.nc.dram_tensor(name, list(shape), dt, kind="ExternalInput").ap()

    def dout(self, name, shape, dt=F32):
        return self.nc.dram_tensor(name, list(shape), dt, kind="ExternalOutput").ap()

    def store(self, eng, out, in_, reads=()):
        o = self.em.dma(eng, out, in_, reads=reads)
        self.outs.append(o)
        return o

    def close(self):
        self.em.finish(self.outs)
        for g in reversed(self.cm):
            g.__exit__(None, None, None)
        return self.nc


def tile_w(w):
    ncol = w.shape[1]
    nblk = ncol // BW
    return np.ascontiguousarray(w.reshape(4, 8, 128, nblk, BW).transpose(3, 0, 2, 1, 4))


def vec_pj(v):
    return np.ascontiguousarray(v.reshape(NJ, 128).T)


MODC = 3 * D // NCORE
MODB = MODC // BW


def build_A():
    kb = KB()
    em = kb.em
    ccT = kb.din("ccT", [128, NJ, 5])
    adaw = kb.din("adaw", [2, MODB, 4, 128, 8, BW])
    adab = kb.din("adab", [2, MODC])
    modo = kb.dout("modo", [5, 2, MODC])
    cc = kb.sb("cc", [128, NJ, 5])
    sc = kb.sb("sc", [128, NJ, 5])
    bias = kb.sb("bias", [5, 2, MODC])
    o = kb.sb("o", [5, 2, MODC])
    wst = [kb.sb("wst%d" % i, [128, 8, BW]) for i in range(3)]
    pp = [kb.ps("P:pp%d" % i, [5, BW]) for i in range(2)]
    em.dma("sp", cc[:], ccT, writes=["cc"])
    for l in range(2):
        em.dma("sp", bias[:, l, :], adab[l].partition_broadcast(5), writes=["bias"])
    em.op("act", lambda e: e.activation(out=sc[:], in_=cc[:], func=AF.Silu), reads=["cc"], writes=["sc"])
    n = 0
    for l in range(2):
        for b in range(MODB):
            p = pp[(l * MODB + b) % 2]
            pk = "P:pp%d" % ((l * MODB + b) % 2)
            for pc in range(4):
                w = wst[n % 3]
                wk = "wst%d" % (n % 3)
                n += 1
                em.dma("sp", w[:], adaw[l, b, pc], writes=[wk])
                for jj in range(8):
                    j = pc * 8 + jj
                    em.op("pe", lambda e, p=p, w=w, j=j, jj=jj: e.matmul(p[:], lhsT=sc[:, j, :], rhs=w[:, jj, :], start=(j == 0), stop=(j == NJ - 1)),
                          reads=["sc", wk], writes=[pk])
            em.op("dve", lambda e, p=p, l=l, b=b: e.tensor_tensor(out=o[:, l, b * BW:(b + 1) * BW], in0=p[:], in1=bias[:, l, b * BW:(b + 1) * BW], op=ALU.add),
                  reads=[pk, "bias"], writes=["o"])
    kb.store("sp", modo, o[:], reads=["o"])
    return kb.close()


def run_A(inp):
    cc = np.concatenate([inp["c"], inp["c_ctx"][None]], 0)
    ccT = np.ascontiguousarray(cc.T.reshape(NJ, 128, 5).transpose(1, 0, 2))
    maps = []
    for r in range(NCORE):
        sl = slice(r * MODC, (r + 1) * MODC)
        aw = np.stack([tile_w(inp["ada_w"][l][:, sl]) for l in range(2)])
        ab = np.ascontiguousarray(inp["ada_b"][:, sl])
        maps.append({"ccT": ccT, "adaw": aw, "adab": ab})
    res = run_bass_kernel_spmd(build_A(), maps, core_ids=list(range(NCORE)))
    mod = np.concatenate([res.results[r]["modo"] for r in range(NCORE)], axis=2)
    return mod


def chunks(n, step):
    return [(a, min(step, n - a)) for a in range(0, n, step)]


def emit_range_reduce(kb, t, tk, kf, ki, add):
    em = kb.em
    kfk, kik = "rr_kf", "rr_ki"
    if add != 0.0:
        em.op("dve", lambda e: e.tensor_scalar(out=t, in0=t, scalar1=float(add), scalar2=None, op0=ALU.add), reads=[tk], writes=[tk])
    em.op("dve", lambda e: e.tensor_scalar(out=kf[:], in0=t, scalar1=float(1 / (2 * PI)), scalar2=None, op0=ALU.mult), reads=[tk], writes=[kfk])
    em.op("dve", lambda e: e.tensor_copy(out=ki[:], in_=kf[:]), reads=[kfk], writes=[kik])
    em.op("dve", lambda e: e.tensor_copy(out=kf[:], in_=ki[:]), reads=[kik], writes=[kfk])
    em.op("dve", lambda e: e.scalar_tensor_tensor(out=t, in0=kf[:], scalar=float(-2 * PI), in1=t, op0=ALU.mult, op1=ALU.add), reads=[kfk, tk], writes=[tk])
    em.op("dve", lambda e: e.tensor_scalar(out=kf[:], in0=t, scalar1=PI, scalar2=float(-2 * PI), op0=ALU.is_gt, op1=ALU.mult), reads=[tk], writes=[kfk])
    em.op("dve", lambda e: e.tensor_tensor(out=t, in0=t, in1=kf[:], op=ALU.add), reads=[tk, kfk], writes=[tk])
    em.op("dve", lambda e: e.tensor_scalar(out=kf[:], in0=t, scalar1=-PI, scalar2=float(2 * PI), op0=ALU.is_lt, op1=ALU.mult), reads=[tk], writes=[kfk])
    em.op("dve", lambda e: e.tensor_tensor(out=t, in0=t, in1=kf[:], op=ALU.add), reads=[tk, kfk], writes=[tk])


def emit_stats(kb, xT, ntok, rbc, ones_mat, epsb, tag):
    em = kb.em
    m0 = kb.mark()
    xs = [kb.sb("st_x%d_%s" % (i, tag), [128, ntok]) for i in range(2)]
    sq = [kb.sb("st_q%d_%s" % (i, tag), [128, ntok]) for i in range(2)]
    cks = chunks(ntok, 512)
    stat = kb.ps("st_ps_" + tag, [128, 512 * len(cks)])
    for j in range(NJ):
        b = j % 2
        em.dma("sp", xs[b][:], xT[j], writes=["st_x%d" % b])
        em.op("act", lambda e, b=b: e.activation(out=sq[b][:], in_=xs[b][:], func=AF.Square), reads=["st_x%d" % b], writes=["st_q%d" % b])
        for ci, (a, n) in enumerate(cks):
            em.op("pe", lambda e, b=b, a=a, n=n, ci=ci, j=j: e.matmul(stat[:, ci * 512:ci * 512 + n], lhsT=ones_mat, rhs=sq[b][:, a:a + n], start=(j == 0), stop=(j == NJ - 1)),
                  reads=["st_q%d" % b, "consts"], writes=["P:st_ps"])
    for ci, (a, n) in enumerate(cks):
        em.op("act", lambda e, a=a, n=n, ci=ci: e.activation(out=rbc[:, a:a + n], in_=stat[:, ci * 512:ci * 512 + n], func=AF.Sqrt, bias=epsb, scale=1.0 / D),
              reads=["P:st_ps", "consts"], writes=["rbc"])
    em.op("dve", lambda e: e.reciprocal(out=rbc, in_=rbc), reads=["rbc"], writes=["rbc"])
    kb.release(m0)


def emit_wblock(kb, wd, blk, st, wst, wb):
    em = kb.em
    bi = st["n"] % 2
    st["n"] += 1
    for pc in range(4):
        si = st["s"] % len(wst)
        st["s"] += 1
        em.dma("sp", wst[si][:], wd[blk, pc], writes=["wst%d" % si])
        em.op("pool", lambda e, si=si, bi=bi, pc=pc: e.tensor_copy(out=wb[bi][:, pc * 8:(pc + 1) * 8, :], in_=wst[si][:]),
              reads=["wst%d" % si], writes=["wb%d" % bi])
    return bi


NTH = NT // 2
TB = 384


def emit_consts(kb, cst):
    em = kb.em
    c32 = kb.sb("c32", [128, 4, 128])
    c16 = kb.sb("c16", [128, 4, 128], BF16)
    epsb = kb.sb("epsb", [128, 1])
    em.dma("sp", c32[:], cst, writes=["consts"])
    em.op("dve", lambda e: e.tensor_copy(out=c16[:], in_=c32[:]), reads=["consts"], writes=["consts16"])
    em.op("pool", lambda e: e.memset(epsb[:], EPS), writes=["consts"])
    return c32, c16, epsb


def make_consts():
    c = np.zeros((128, 4, 128), np.float32)
    c[:, 0, :] = np.eye(128)
    R = np.zeros((128, 128), np.float32)
    for m in range(128):
        if (m % 64) < 32:
            R[m, m + 32] = -1.0
        else:
            R[m, m - 32] = 1.0
    c[:, 1, :] = R.T
    c[:, 2, :] = 1.0
    c[:, 3, 0] = np.arange(128) % 32
    return c


def emit_prenorm_half(kb, xT, c0, n, rbc, hT, s1, s2, xh):
    em = kb.em
    for j in range(NJ):
        b = j % 2
        em.dma("sp", xh[b][:, 0:n], xT[j, :, c0:c0 + n], writes=["pn_x%d" % b])
        em.op("dve", lambda e, b=b: e.tensor_tensor(out=xh[b][:, 0:n], in0=xh[b][:, 0:n], in1=rbc[:, c0:c0 + n], op=ALU.mult),
              reads=["pn_x%d" % b, "rbc"], writes=["pn_x%d" % b])
        nc_ = max(0, min(NCTX, c0 + n) - c0)
        if nc_ > 0:
            em.op("dve", lambda e, b=b, j=j, nc_=nc_: e.tensor_scalar(out=hT[:, j, 0:nc_], in0=xh[b][:, 0:nc_], scalar1=s1[:, 1, j:j + 1], scalar2=s2[:, 1, j:j + 1], op0=ALU.mult, op1=ALU.add),
                  reads=["pn_x%d" % b, "svec"], writes=["hT"])
        if nc_ < n:
            em.op("dve", lambda e, b=b, j=j, nc_=nc_: e.tensor_scalar(out=hT[:, j, nc_:n], in0=xh[b][:, nc_:n], scalar1=s1[:, 0, j:j + 1], scalar2=s2[:, 0, j:j + 1], op0=ALU.mult, op1=ALU.add),
                  reads=["pn_x%d" % b, "svec"], writes=["hT"])


def emit_svec(kb, vec):
    em = kb.em
    v = kb.sb("vecs", [128, 5, NJ])
    s1 = kb.sb("s1", [128, 2, NJ])
    s2 = kb.sb("s2", [128, 2, NJ])
    em.dma("sp", v[:], vec, writes=["vecs"])
    for i in range(2):
        em.op("dve", lambda e, i=i: e.scalar_tensor_tensor(out=s1[:, i, :], in0=v[:, 2 + 2 * i, :], scalar=1.0, in1=v[:, 0, :], op0=ALU.add, op1=ALU.mult),
              reads=["vecs"], writes=["svec"])
        em.op("dve", lambda e, i=i: e.tensor_copy(out=s2[:, i, :], in_=v[:, 1 + 2 * i, :]), reads=["vecs"], writes=["svec"])
    return s1, s2


def build_B(stage=9, blks=None, halves=(0, 1), nwin=4 * D // BW, steng='act'):
    kb = KB()
    if blks is None:
        blks = list(range(4 * D // BW))
    em = kb.em
    xT = kb.din("xT", [NJ, 128, NT])
    vec = kb.din("vec", [128, 5, NJ])
    win = kb.din("win", [nwin, 4, 128, 8, BW])
    pos = kb.din("pos", [2, NT])
    cst = kb.din("cst", [128, 4, 128])
    qT = kb.dout("qT", [32, 128, NT], BF16)
    kT = kb.dout("kT", [32, 128, NT], BF16)
    vo = kb.dout("v", [NT, D], BF16)
    go = kb.dout("g", [NT, D], BF16)
    c32, c16, epsb = emit_consts(kb, cst)
    s1, s2 = emit_svec(kb, vec)
    rbc = kb.sb("rbc", [128, NT])
    emit_stats(kb, xT, NT, rbc[:], c32[:, 2, :], epsb[:, :], "b")
    if stage == 1:
        dbg = kb.dout("dbg", [128, NT])
        kb.store("sp", dbg, rbc[:], reads=["rbc"])
        return kb.close()
    cosT = kb.sb("cosT", [128, NT])
    sinT = kb.sb("sinT", [128, NT])
    invf = kb.sb("invf", [128, 1])
    em.op("act", lambda e: e.activation(out=invf[:], in_=c32[:, 3, 0:1], func=AF.Exp, scale=float(-np.log(10000.0) / 32)), reads=["consts"], writes=["invf"])
    em.dma("sp", sinT[0:64, :], pos[0].partition_broadcast(64), writes=["sinT"])
    em.dma("sp", sinT[64:128, :], pos[1].partition_broadcast(64), writes=["sinT"])
    em.op("dve", lambda e: e.tensor_scalar(out=sinT[:], in0=sinT[:], scalar1=invf[:, 0:1], scalar2=None, op0=ALU.mult), reads=["sinT", "invf"], writes=["sinT"])
    em.op("dve", lambda e: e.tensor_copy(out=cosT[:], in_=sinT[:]), reads=["sinT"], writes=["cosT"])
    m1 = kb.mark()
    kf = kb.sb("rr_kf", [128, NT])
    ki = kb.sb("rr_ki", [128, NT], I32)
    emit_range_reduce(kb, sinT[:], "sinT", kf, ki, 0.0)
    emit_range_reduce(kb, cosT[:], "cosT", kf, ki, PI / 2)
    em.op("act", lambda e: e.activation(out=sinT[:], in_=sinT[:], func=AF.Sin), reads=["sinT"], writes=["sinT"])
    em.op("act", lambda e: e.activation(out=cosT[:], in_=cosT[:], func=AF.Sin), reads=["cosT"], writes=["cosT"])
    kb.release(m1)
    if stage == 2:
        dbg = kb.dout("dbg", [2, 128, NT])
        kb.store("sp", dbg[0], cosT[:], reads=["cosT"])
        kb.store("sp", dbg[1], sinT[:], reads=["sinT"])
        return kb.close()
    hT = kb.sb("hT", [128, NJ, NTH], BF16)
    xh = [kb.sb("pn_x%d" % i, [128, NTH]) for i in range(2)]
    wst = [kb.sb("wst%d" % i, [128, 8, BW]) for i in range(2)]
    wb = [kb.sb("wb%d" % i, [128, NJ, BW], BF16) for i in range(2)]
    pacc = [kb.ps("P:pacc%d" % i, [128, 512]) for i in range(3)]
    prot = [kb.ps("P:prot%d" % i, [128, 512]) for i in range(2)]
    qb = [kb.sb("qb%d" % i, [128, TB], BF16) for i in range(2)]
    t1 = [kb.sb("t1%d" % i, [128, TB]) for i in range(2)]
    t2 = [kb.sb("t2%d" % i, [128, TB]) for i in range(2)]
    ostg = [kb.sb("ostg%d" % i, [128, NTH], BF16) for i in range(2)]
    vstg = [kb.sb("vstg%d" % i, [128, BW], BF16) for i in range(3)]
    st = {"n": 0, "s": 0}
    na = 0
    nq = 0
    no = 0
    nv = 0
    for half in halves:
        c0 = half * NTH
        emit_prenorm_half(kb, xT, c0, NTH, rbc, hT, s1, s2, xh)
        if stage == 66:
            em.op("dve", lambda e: e.tensor_copy(out=t1[0][:], in_=rbc[:, 0:TB]), reads=["rbc"], writes=["t10"])
            dbg = kb.dout("dbg", [128, TB])
            kb.store("sp", dbg, t1[0][:], reads=["t10"])
            return kb.close()
        if stage == 67:
            em.op("dve", lambda e: e.tensor_copy(out=t1[0][:], in_=rbc[:, 0:TB]), reads=["rbc", "hT"], writes=["t10"])
            dbg = kb.dout("dbg", [128, TB])
            kb.store("sp", dbg, t1[0][:], reads=["t10"])
            return kb.close()
        if stage == 3:
            dbg = kb.dout("dbg", [128, NJ, NTH], BF16)
            kb.store("sp", dbg, hT[:], reads=["hT"])
            return kb.close()
        for blk in blks:
            bi = emit_wblock(kb, win, blk, st, wst, wb)
            wk = "wb%d" % bi
            col0 = blk * BW
            if stage == 4:
                dbg = kb.dout("dbg", [128, NJ, BW], BF16)
                kb.store("sp", dbg, wb[bi][:], reads=[wk])
                return kb.close()
            if col0 < 2 * D:
                for sub in range(BW // 128):
                    cb = (col0 + sub * 128) // 128
                    dst = qT if cb < 32 else kT
                    og = ostg[no % 2]
                    ok = "ostg%d" % (no % 2)
                    no += 1
                    for tb in range(NTH // TB):
                        p = pacc[na % 3]
                        pk = "P:pacc%d" % (na % 3)
                        na += 1
                        for j in range(NJ):
                            em.op("pe", lambda e, p=p, j=j, bi=bi, sub=sub, tb=tb: e.matmul(p[:, 0:TB], lhsT=wb[bi][:, j, sub * 128:(sub + 1) * 128], rhs=hT[:, j, tb * TB:(tb + 1) * TB], start=(j == 0), stop=(j == NJ - 1)),
                                  reads=[wk, "hT"], writes=[pk])
                        qi = nq % 2
                        nq += 1
                        tok = slice(c0 + tb * TB, c0 + (tb + 1) * TB)
                        em.op("act", lambda e, p=p, qi=qi: e.copy(out=qb[qi][:], in_=p[:, 0:TB]), reads=[pk], writes=["qb%d" % qi])
                        if stage == 5:
                            dbg = kb.dout("dbg", [128, TB], BF16)
                            kb.store("sp", dbg, qb[qi][:], reads=["qb%d" % qi])
                            return kb.close()
                        pr = prot[qi]
                        if stage == 68:
                            em.op("dve", lambda e, p=p, qi=qi, tok=tok: e.tensor_tensor(out=t1[qi][:], in0=p[:, 0:TB], in1=rbc[:, tok], op=ALU.mult),
                                  reads=[pk, "rbc", "qb%d" % qi], writes=["t1%d" % qi])
                        if stage == 63:
                            em.op("dve", lambda e, p=p, qi=qi, tok=tok: e.tensor_tensor(out=t1[qi][:], in0=p[:, 0:TB], in1=rbc[:, tok], op=ALU.mult),
                                  reads=[pk, "rbc"], writes=["t1%d" % qi])
                        if stage == 64:
                            em.op("act", lambda e, p=p, qi=qi: e.copy(out=t2[qi][:], in_=p[:, 0:TB]), reads=[pk], writes=["t2%d" % qi])
                            em.op("dve", lambda e, p=p, qi=qi, tok=tok: e.tensor_tensor(out=t1[qi][:], in0=t2[qi][:], in1=cosT[:, tok], op=ALU.mult),
                                  reads=["t2%d" % qi, "cosT"], writes=["t1%d" % qi])
                        if stage == 65:
                            em.op("pool", lambda e, p=p, qi=qi, tok=tok: e.tensor_tensor(out=t1[qi][:], in0=qb[qi][:], in1=cosT[:, tok], op=ALU.mult),
                                  reads=["qb%d" % qi, "cosT"], writes=["t1%d" % qi])
                        if stage == 61:
                            em.op("dve", lambda e, p=p, qi=qi, tok=tok: e.tensor_tensor(out=t1[qi][:], in0=p[:, 0:TB], in1=cosT[:, tok], op=ALU.mult),
                                  reads=[pk, "cosT"], writes=["t1%d" % qi])
                        if stage in (61, 63, 64, 65, 68):
                            dbg = kb.dout("dbg", [128, TB])
                            kb.store("sp", dbg, t1[qi][:], reads=["t1%d" % qi])
                            return kb.close()
                        em.op("pe", lambda e, pr=pr, qi=qi: e.matmul(pr[:, 0:TB], lhsT=c16[:, 1, :], rhs=qb[qi][:], start=True, stop=True),
                              reads=["qb%d" % qi, "consts16"], writes=["P:prot%d" % qi])
                        if stage == 62:
                            em.op("act", lambda e, pr=pr, qi=qi: e.copy(out=t2[qi][:], in_=pr[:, 0:TB]), reads=["P:prot%d" % qi], writes=["t2%d" % qi])
                            dbg = kb.dout("dbg", [128, TB])
                            kb.store("sp", dbg, t2[qi][:], reads=["t2%d" % qi])
                            return kb.close()
                        em.op("dve", lambda e, p=p, qi=qi, tok=tok: e.tensor_tensor(out=t1[qi][:], in0=p[:, 0:TB], in1=cosT[:, tok], op=ALU.mult),
                              reads=[pk, "cosT"], writes=["t1%d" % qi])
                        em.op("dve", lambda e, pr=pr, qi=qi, tok=tok: e.tensor_tensor(out=t2[qi][:], in0=pr[:, 0:TB], in1=sinT[:, tok], op=ALU.mult),
                              reads=["P:prot%d" % qi, "sinT"], writes=["t2%d" % qi])
                        if stage == 6:
                            dbg = kb.dout("dbg", [2, 128, TB])
                            kb.store("sp", dbg[0], t1[qi][:], reads=["t1%d" % qi])
                            kb.store("sp", dbg[1], t2[qi][:], reads=["t2%d" % qi])
                            return kb.close()
                        em.op("pool", lambda e, qi=qi, og=og, tb=tb: e.tensor_tensor(out=og[:, tb * TB:(tb + 1) * TB], in0=t1[qi][:], in1=t2[qi][:], op=ALU.add),
                              reads=["t1%d" % qi, "t2%d" % qi], writes=[ok])
                        if stage == 7:
                            dbg = kb.dout("dbg", [128, NTH], BF16)
                            kb.store("sp", dbg, og[:], reads=[ok])
                            return kb.close()
                    kb.store(steng, dst[cb % 32, :, c0:c0 + NTH], og[:], reads=[ok])
            else:
                dst = vo if col0 < 3 * D else go
                dc = col0 - (2 * D if col0 < 3 * D else 3 * D)
                for tt in range(NTH // 128):
                    p = pacc[na % 3]
                    pk = "P:pacc%d" % (na % 3)
                    na += 1
                    for j in range(NJ):
                        em.op("pe", lambda e, p=p, j=j, bi=bi, tt=tt: e.matmul(p[:, 0:BW], lhsT=hT[:, j, tt * 128:(tt + 1) * 128], rhs=wb[bi][:, j, :], start=(j == 0), stop=(j == NJ - 1)),
                              reads=[wk, "hT"], writes=[pk])
                    vi = nv % 3
                    nv += 1
                    em.op("act", lambda e, p=p, vi=vi: e.copy(out=vstg[vi][:], in_=p[:, 0:BW]), reads=[pk], writes=["vstg%d" % vi])
                    kb.store(steng, dst[c0 + tt * 128:c0 + (tt + 1) * 128, dc:dc + BW], vstg[vi][:], reads=["vstg%d" % vi])
    return kb.close()


def core_layout(inp):
    outs = []
    for r in range(NCORE):
        b, hf = r // 2, r % 2
        lat = inp["x"][b, hf * NLAT:(hf + 1) * NLAT]
        ctx = inp["ctx"][b]
        tid = np.arange(hf * NLAT, (hf + 1) * NLAT)
        if hf == 1:
            lat = lat[::-1]
            ctx = ctx[::-1]
            tid = tid[::-1]
        xc = np.concatenate([ctx, lat], 0)
        xT = np.ascontiguousarray(xc.T.reshape(NJ, 128, NT))
        pos = np.zeros((2, NT), np.float32)
        pos[0, NCTX:] = tid // 64
        pos[1, NCTX:] = tid % 64
        outs.append({"xT": xT, "pos": pos})
    return outs


def mod_vec(mod, l, b, norm_w):
    m = mod[:, l]
    return np.ascontiguousarray(np.stack([vec_pj(norm_w), vec_pj(m[b, 0:D]), vec_pj(m[b, D:2 * D]), vec_pj(m[4, 0:D]), vec_pj(m[4, D:2 * D])], 1))


def run_B(inp, mod, lay):
    win = tile_w(inp["attn_w_in"][0])
    cst = make_consts()
    maps = []
    for r in range(NCORE):
        maps.append({"xT": lay[r]["xT"], "vec": mod_vec(mod, 0, r // 2, inp["norm_pre"][0]), "win": win, "pos": lay[r]["pos"], "cst": cst})
    res = run_bass_kernel_spmd(build_B(), maps, core_ids=list(range(NCORE)))
    return res.results


def emit_outproj_residual(kb, actT, wd, xT, gn, outT, c32, epsb, ncols, col_off, tagp):
    em = kb.em
    nc = kb.nc
    o2T = nc.dram_tensor("o2T_" + tagp, [NJ, 128, ncols], F32).ap()
    nh = ncols // 2
    tb_ = nh // 3 if nh % 3 == 0 and nh // 3 <= 512 else 512
    tbs = chunks(nh, tb_)
    m0 = kb.mark()
    aT = kb.sb("op_aT", [128, NJ, nh], BF16)
    wst = [kb.sb("op_wst%d" % i, [128, 8, BW]) for i in range(2)]
    wb = [kb.sb("op_wb%d" % i, [128, NJ, BW], BF16) for i in range(2)]
    rbc2 = kb.sb("op_rbc2", [128, ncols])
    o2s = [kb.sb("op_o2s%d" % i, [128, 512]) for i in range(2)]
    sqs = [kb.sb("op_sqs%d" % i, [128, 512]) for i in range(2)]
    pacc = [kb.ps("P:op_pacc%d" % i, [128, 512]) for i in range(3)]
    stat = kb.ps("P:op_stat", [128, len(tbs), 512])
    st = {"n": 0, "s": 0}
    na = 0
    ns = 0
    for half in range(2):
        c0 = half * nh
        for j in range(NJ):
            em.dma("sp", aT[:, j, :], actT[j, :, c0:c0 + nh], reads=[("actT", j)], writes=["op_aT"])
        for blk in range(D // BW):
            em2 = em
            bi = st["n"] % 2
            st["n"] += 1
            for pc in range(4):
                si = st["s"] % 2
                st["s"] += 1
                em.dma("sp", wst[si][:], wd[blk, pc], writes=["op_wst%d" % si])
                em.op("pool", lambda e, si=si, bi=bi, pc=pc: e.tensor_copy(out=wb[bi][:, pc * 8:(pc + 1) * 8, :], in_=wst[si][:]),
                      reads=["op_wst%d" % si], writes=["op_wb%d" % bi])
            wk = "op_wb%d" % bi
            for sub in range(BW // 128):
                fb = blk * (BW // 128) + sub
                for ti, (ta, tn) in enumerate(tbs):
                    p = pacc[na % 3]
                    pk = "P:op_pacc%d" % (na % 3)
                    na += 1
                    for j in range(NJ):
                        em.op("pe", lambda e, p=p, j=j, bi=bi, sub=sub, ta=ta, tn=tn: e.matmul(p[:, 0:tn], lhsT=wb[bi][:, j, sub * 128:(sub + 1) * 128], rhs=aT[:, j, ta:ta + tn], start=(j == 0), stop=(j == NJ - 1)),
                              reads=[wk, "op_aT"], writes=[pk])
                    si = ns % 2
                    ns += 1
                    em.op("act", lambda e, p=p, si=si, tn=tn: e.copy(out=o2s[si][:, 0:tn], in_=p[:, 0:tn]), reads=[pk], writes=["op_o2s%d" % si])
                    em.op("act", lambda e, p=p, si=si, tn=tn: e.activation(out=sqs[si][:, 0:tn], in_=p[:, 0:tn], func=AF.Square), reads=[pk], writes=["op_sqs%d" % si])
                    em.dma("act", o2T[fb, :, c0 + ta:c0 + ta + tn], o2s[si][:, 0:tn], reads=["op_o2s%d" % si], writes=[("o2T", fb)])
                    em.op("pe", lambda e, si=si, ti=ti, tn=tn, fb=fb: e.matmul(stat[:, ti, 0:tn], lhsT=c32[:, 2, :], rhs=sqs[si][:, 0:tn], start=(fb == 0), stop=(fb == NJ - 1)),
                          reads=["op_sqs%d" % si, "consts"], writes=["P:op_stat"])
        for ti, (ta, tn) in enumerate(tbs):
            em.op("act", lambda e, ti=ti, ta=ta, tn=tn, c0=c0: e.activation(out=rbc2[:, c0 + ta:c0 + ta + tn], in_=stat[:, ti, 0:tn], func=AF.Sqrt, bias=epsb[:, :], scale=1.0 / D),
                  reads=["P:op_stat", "consts"], writes=["op_rbc2"])
    em.op("dve", lambda e: e.reciprocal(out=rbc2[:], in_=rbc2[:]), reads=["op_rbc2"], writes=["op_rbc2"])
    xj = [kb.sb("op_xj%d" % i, [128, ncols]) for i in range(2)]
    oj = [kb.sb("op_oj%d" % i, [128, ncols]) for i in range(2)]
    nctx = max(0, NCTX - col_off)
    for j in range(NJ):
        b = j % 2
        em.dma("sp", xj[b][:], xT[j, :, col_off:col_off + ncols], writes=["op_xj%d" % b])
        em.dma("sp", oj[b][:], o2T[j], reads=[("o2T", j)], writes=["op_oj%d" % b])
        em.op("dve", lambda e, b=b: e.tensor_tensor(out=oj[b][:], in0=oj[b][:], in1=rbc2[:], op=ALU.mult), reads=["op_oj%d" % b, "op_rbc2"], writes=["op_oj%d" % b])
        if nctx > 0:
            em.op("dve", lambda e, b=b, j=j: e.scalar_tensor_tensor(out=xj[b][:, 0:nctx], in0=oj[b][:, 0:nctx], scalar=gn[:, 1, j:j + 1], in1=xj[b][:, 0:nctx], op0=ALU.mult, op1=ALU.add),
                  reads=["op_oj%d" % b, "op_xj%d" % b, "gn"], writes=["op_xj%d" % b])
        em.op("dve", lambda e, b=b, j=j: e.scalar_tensor_tensor(out=xj[b][:, nctx:], in0=oj[b][:, nctx:], scalar=gn[:, 0, j:j + 1], in1=xj[b][:, nctx:], op0=ALU.mult, op1=ALU.add),
              reads=["op_oj%d" % b, "op_xj%d" % b, "gn"], writes=["op_xj%d" % b])
        kb.store("act", outT[j], xj[b][:], reads=["op_xj%d" % b])
    kb.release(m0)


def emit_gn(kb, vec2):
    em = kb.em
    v = kb.sb("vec2s", [128, 3, NJ])
    gn = kb.sb("gn", [128, 2, NJ])
    em.dma("sp", v[:], vec2, writes=["vec2"])
    for i in range(2):
        em.op("dve", lambda e, i=i: e.tensor_tensor(out=gn[:, i, :], in0=v[:, 1 + i, :], in1=v[:, 0, :], op=ALU.mult), reads=["vec2"], writes=["gn"])
    return gn


def mod_vec2(mod, l, b, norm_w):
    m = mod[:, l]
    return np.ascontiguousarray(np.stack([vec_pj(norm_w), vec_pj(m[b, 2 * D:3 * D]), vec_pj(m[4, 2 * D:3 * D])], 1))


NK = NCTX + 2 * NLAT
NKC = NK // 128
SCQ = float(128 ** -0.5)
LAM_INIT0 = 0.8 - 0.6 * float(np.exp(-0.3 * 0))


def build_C():
    kb = KB()
    em = kb.em
    nc = kb.nc
    qT = kb.din("qT", [32, 128, NT], BF16)
    kTa = kb.din("kTa", [32, 128, NK], BF16)
    va = kb.din("va", [NK, D], BF16)
    gd = kb.din("g", [NT, D], BF16)
    xT = kb.din("xT", [NJ, 128, NT])
    wout = kb.din("wout", [D // BW, 4, 128, 8, BW])
    vec2 = kb.din("vec2", [128, 3, NJ])
    subln = kb.din("subln", [128, 256])
    lamv = kb.din("lamv", [128, 4, 128])
    cst = kb.din("cst", [128, 4, 128])
    x1T = kb.dout("x1T", [NJ, 128, NT])
    yT = nc.dram_tensor("yT_c", [NJ, 128, NT], BF16).ap()
    c32, c16, epsb = emit_consts(kb, cst)
    gn = emit_gn(kb, vec2)
    lv = kb.sb("lv", [128, 4, 128])
    lp = kb.sb("lp", [128, 2, 128])
    ls = kb.sb("ls", [128, 4])
    subg = kb.sb("subg", [128, 256])
    em.dma("sp", lv[:], lamv, writes=["lv"])
    em.dma("sp", subg[:], subln, writes=["subg"])
    em.op("dve", lambda e: e.tensor_tensor(out=lp[:], in0=lv[:, 0:4:2, :], in1=lv[:, 1:4:2, :], op=ALU.mult), reads=["lv"], writes=["lp"])
    em.op("dve", lambda e: e.reduce_sum(out=ls[:, 0:2], in_=lp[:], axis=AX.X), reads=["lp"], writes=["ls"])
    em.op("act", lambda e: e.activation(out=ls[:, 0:2], in_=ls[:, 0:2], func=AF.Exp), reads=["ls"], writes=["ls"])
    em.op("dve", lambda e: e.tensor_tensor(out=ls[:, 2:3], in0=ls[:, 1:2], in1=ls[:, 0:1], op=ALU.subtract), reads=["ls"], writes=["ls"])
    em.op("dve", lambda e: e.tensor_scalar(out=ls[:, 3:4], in0=ls[:, 2:3], scalar1=-LAM_INIT0, scalar2=None, op0=ALU.add), reads=["ls"], writes=["ls"])
    em.op("dve", lambda e: e.tensor_scalar(out=subg[:], in0=subg[:], scalar1=1.0 - LAM_INIT0, scalar2=None, op0=ALU.mult), reads=["subg"], writes=["subg"])
    nlam = ls[:, 3:4]
    m0 = kb.mark()
    kh = [kb.sb("kh%d" % i, [128, 2, NK], BF16) for i in range(2)]
    qh = [kb.sb("qh%d" % i, [128, 2, NT], BF16) for i in range(2)]
    V1 = [kb.sb("V1%d" % i, [128, NKC, 257], BF16) for i in range(2)]
    gh = [kb.sb("gh%d" % i, [128, NT // 128, 256], BF16) for i in range(2)]
    pT = [kb.sb("pT%d" % i, [128, 512], BF16) for i in range(3)]
    on0 = [kb.sb("on0%d" % i, [128, 256]) for i in range(4)]
    oo = [kb.sb("oo%d" % i, [128, 256]) for i in range(2)]
    sqt = [kb.sb("sqt%d" % i, [128, 256]) for i in range(2)]
    sgt = [kb.sb("sgt%d" % i, [128, 256]) for i in range(2)]
    ybt = [kb.sb("ybt%d" % i, [128, 256], BF16) for i in range(2)]
    yts = [kb.sb("yts%d" % i, [128, 2, 128], BF16) for i in range(2)]
    sm = [kb.sb("sm%d" % i, [128, 4]) for i in range(4)]
    psc = [kb.ps("P:psc%d" % i, [128, 512]) for i in range(2)]
    acc = [kb.ps("P:acc%d" % i, [128, 512]) for i in range(4)]
    ptr = kb.ps("ptr", [128, 2, 2, 128], BF16)
    for i in range(2):
        em.op("pool", lambda e, i=i: e.memset(V1[i][:, :, 256:257], 1.0), writes=["V1%d" % i])
    nsc = 0
    npt = 0
    nfin = 0
    for h in range(16):
        hb = h % 2
        for n in range(2):
            em.dma("sp", kh[hb][:, n, :], kTa[2 * h + n], writes=["kh%d" % hb])
            em.dma("sp", qh[hb][:, n, :], qT[2 * h + n], writes=["qh%d" % hb])
        em.dma("sp", V1[hb][:, :, 0:256], va[:, h * 256:(h + 1) * 256].rearrange("(c p) v -> p c v", p=128), writes=["V1%d" % hb])
        em.dma("sp", gh[hb][:], gd[:, h * 256:(h + 1) * 256].rearrange("(c p) v -> p c v", p=128), writes=["gh%d" % hb])
        groups = [(0, NCTX, [0, 1])] + [(NCTX + 512 * i, 512, list(range(NKC))) for i in range(NLAT // 512)]
        for (q0, nq, kcs) in groups:
            nqs = nq // 128
            for n in range(2):
                for kc in kcs:
                    si = nsc % 2
                    nsc += 1
                    em.op("pe", lambda e, si=si, hb=hb, n=n, kc=kc, q0=q0, nq=nq: e.matmul(psc[si][:, 0:nq], lhsT=kh[hb][:, n, kc * 128:(kc + 1) * 128], rhs=qh[hb][:, n, q0:q0 + nq], start=True, stop=True),
                          reads=["kh%d" % hb, "qh%d" % hb], writes=["P:psc%d" % si])
                    pi = npt % 3
                    npt += 1
                    em.op("act", lambda e, si=si, pi=pi, nq=nq: e.activation(out=pT[pi][:, 0:nq], in_=psc[si][:, 0:nq], func=AF.Exp, scale=SCQ),
                          reads=["P:psc%d" % si], writes=["pT%d" % pi])
                    for qs in range(nqs):
                        em.op("pe", lambda e, qs=qs, pi=pi, hb=hb, kc=kc, kcs=kcs: e.matmul(acc[qs][:, 0:257], lhsT=pT[pi][:, qs * 128:(qs + 1) * 128], rhs=V1[hb][:, kc, :], start=(kc == kcs[0]), stop=(kc == kcs[-1])),
                              reads=["pT%d" % pi, "V1%d" % hb], writes=["P:acc%d" % qs])
                for qs in range(nqs):
                    s_ = sm[qs]
                    sk = "sm%d" % qs
                    em.op("dve", lambda e, qs=qs, s_=s_: e.reciprocal(out=s_[:, 0:1], in_=acc[qs][:, 256:257]), reads=["P:acc%d" % qs], writes=[sk])
                    if n == 0:
                        em.op("dve", lambda e, qs=qs, s_=s_: e.tensor_scalar(out=on0[qs][:], in0=acc[qs][:, 0:256], scalar1=s_[:, 0:1], scalar2=None, op0=ALU.mult),
                              reads=["P:acc%d" % qs, sk], writes=["on0%d" % qs])
                    else:
                        fi = nfin % 2
                        nfin += 1
                        tt = (q0 + qs * 128) // 128
                        em.op("dve", lambda e, s_=s_: e.tensor_tensor(out=s_[:, 1:2], in0=s_[:, 0:1], in1=nlam, op=ALU.mult), reads=[sk, "ls"], writes=[sk])
                        em.op("dve", lambda e, qs=qs, s_=s_, fi=fi: e.scalar_tensor_tensor(out=oo[fi][:], in0=acc[qs][:, 0:256], scalar=s_[:, 1:2], in1=on0[qs][:], op0=ALU.mult, op1=ALU.add),
                              reads=["P:acc%d" % qs, sk, "on0%d" % qs], writes=["oo%d" % fi])
                        em.op("pool", lambda e, fi=fi: e.tensor_tensor(out=sqt[fi][:], in0=oo[fi][:], in1=oo[fi][:], op=ALU.mult), reads=["oo%d" % fi], writes=["sqt%d" % fi])
                        em.op("dve", lambda e, fi=fi, s_=s_: e.reduce_sum(out=s_[:, 2:3], in_=sqt[fi][:], axis=AX.X), reads=["sqt%d" % fi], writes=[sk])
                        em.op("act", lambda e, s_=s_: e.activation(out=s_[:, 2:3], in_=s_[:, 2:3], func=AF.Sqrt, bias=epsb[:, :], scale=1.0 / 256), reads=[sk, "consts"], writes=[sk])
                        em.op("dve", lambda e, s_=s_: e.reciprocal(out=s_[:, 3:4], in_=s_[:, 2:3]), reads=[sk], writes=[sk])
                        em.op("act", lambda e, fi=fi, hb=hb, tt=tt: e.activation(out=sgt[fi][:], in_=gh[hb][:, tt, :], func=AF.Silu), reads=["gh%d" % hb], writes=["sgt%d" % fi])
                        em.op("dve", lambda e, fi=fi, s_=s_: e.scalar_tensor_tensor(out=oo[fi][:], in0=oo[fi][:], scalar=s_[:, 3:4], in1=subg[:], op0=ALU.mult, op1=ALU.mult),
                              reads=["oo%d" % fi, sk, "subg"], writes=["oo%d" % fi])
                        em.op("pool", lambda e, fi=fi: e.tensor_tensor(out=ybt[fi][:], in0=oo[fi][:], in1=sgt[fi][:], op=ALU.mult), reads=["oo%d" % fi, "sgt%d" % fi], writes=["ybt%d" % fi])
                        for v in range(2):
                            em.op("pe", lambda e, fi=fi, v=v: e.transpose(ptr[:, fi, v, :], ybt[fi][:, v * 128:(v + 1) * 128], c16[:, 0, :]), reads=["ybt%d" % fi, "consts16"], writes=["P:ptr%d" % fi])
                        em.op("act", lambda e, fi=fi: e.copy(out=yts[fi][:], in_=ptr[:, fi, :, :]), reads=["P:ptr%d" % fi], writes=["yts%d" % fi])
                        for v in range(2):
                            em.dma("act", yT[2 * h + v, :, tt * 128:(tt + 1) * 128], yts[fi][:, v, :], reads=["yts%d" % fi], writes=[("actT", 2 * h + v)])
    kb.release(m0)
    emit_outproj_residual(kb, yT, wout, xT, gn, x1T, c32, epsb, NT, 0, "c")
    return kb.close()


def run_C(inp, mod, lay, resB):
    wout = tile_w(inp["attn_w_out"][0])
    cst = make_consts()
    subln = np.ascontiguousarray(np.broadcast_to(inp["attn_subln"][0][None, :], (128, 256)))
    lamv = np.ascontiguousarray(np.broadcast_to(inp["attn_lam"][0][None], (128, 4, 128)))
    maps = []
    for r in range(NCORE):
        pr = r ^ 1
        kTa = np.concatenate([resB[r]["kT"], resB[pr]["kT"][:, :, NCTX:]], axis=2)
        va = np.concatenate([resB[r]["v"], resB[pr]["v"][NCTX:]], axis=0)
        maps.append({"qT": resB[r]["qT"], "kTa": kTa, "va": va, "g": resB[r]["g"], "xT": lay[r]["xT"], "wout": wout,
                     "vec2": mod_vec2(mod, 0, r // 2, inp["norm_post"][0]), "subln": subln, "lamv": lamv, "cst": cst})
    res = run_bass_kernel_spmd(build_C(), maps, core_ids=list(range(NCORE)))
    return res.results


def build_D():
    kb = KB()
    em = kb.em
    xT = kb.din("xT", [NJ, 128, NT])
    vec = kb.din("vec", [128, 5, NJ])
    win = kb.din("win", [2 * D // BW, 4, 128, 8, BW])
    cst = kb.din("cst", [128, 4, 128])
    uT = kb.dout("uT", [NJ, 128, NT])
    szT = kb.dout("szT", [NJ, 128, NT], BF16)
    c32, c16, epsb = emit_consts(kb, cst)
    s1, s2 = emit_svec(kb, vec)
    rbc = kb.sb("rbc", [128, NT])
    emit_stats(kb, xT, NT, rbc[:], c32[:, 2, :], epsb[:, :], "d")
    hT = kb.sb("hT", [128, NJ, NTH], BF16)
    xh = [kb.sb("pn_x%d" % i, [128, NTH]) for i in range(2)]
    wst = [kb.sb("wst%d" % i, [128, 8, BW]) for i in range(2)]
    wb = [kb.sb("wb%d" % i, [128, NJ, BW], BF16) for i in range(2)]
    pacc = [kb.ps("P:pacc%d" % i, [128, 512]) for i in range(3)]
    ustg = [kb.sb("ustg%d" % i, [128, NTH]) for i in range(2)]
    zstg = [kb.sb("zstg%d" % i, [128, NTH], BF16) for i in range(2)]
    st = {"n": 0, "s": 0}
    na = 0
    no = 0
    for half in range(2):
        c0 = half * NTH
        emit_prenorm_half(kb, xT, c0, NTH, rbc, hT, s1, s2, xh)
        for blk in range(2 * D // BW):
            bi = emit_wblock(kb, win, blk, st, wst, wb)
            wk = "wb%d" % bi
            for sub in range(BW // 128):
                cb = (blk * BW + sub * 128) // 128
                isu = cb < 32
                oi = no % 2
                no += 1
                og = ustg[oi] if isu else zstg[oi]
                ok = ("ustg%d" if isu else "zstg%d") % oi
                for tb in range(NTH // TB):
                    p = pacc[na % 3]
                    pk = "P:pacc%d" % (na % 3)
                    na += 1
                    for j in range(NJ):
                        em.op("pe", lambda e, p=p, j=j, bi=bi, sub=sub, tb=tb: e.matmul(p[:, 0:TB], lhsT=wb[bi][:, j, sub * 128:(sub + 1) * 128], rhs=hT[:, j, tb * TB:(tb + 1) * TB], start=(j == 0), stop=(j == NJ - 1)),
                              reads=[wk, "hT"], writes=[pk])
                    if isu:
                        em.op("act", lambda e, p=p, og=og, tb=tb: e.copy(out=og[:, tb * TB:(tb + 1) * TB], in_=p[:, 0:TB]), reads=[pk], writes=[ok])
                    else:
                        em.op("act", lambda e, p=p, og=og, tb=tb: e.activation(out=og[:, tb * TB:(tb + 1) * TB], in_=p[:, 0:TB], func=AF.Silu), reads=[pk], writes=[ok])
                dst = uT if isu else szT
                kb.store("act", dst[cb % 32, :, c0:c0 + NTH], og[:], reads=[ok])
    return kb.close()


def run_D(inp, mod, x1):
    win = tile_w(inp["s5_w_in"][0])
    cst = make_consts()
    maps = []
    for r in range(NCORE):
        maps.append({"xT": x1[r], "vec": mod_vec(mod, 1, r // 2, inp["norm_pre"][1]), "win": win, "cst": cst})
    res = run_bass_kernel_spmd(build_D(), maps, core_ids=list(range(NCORE)))
    return res.results


NGP = 128
NSTEP = 12


def s5_params(inp, dirn):
    def gp(a):
        return a.reshape(NGP, 2, 64).transpose(1, 2, 0).reshape(128, NGP)
    ldt = np.broadcast_to(inp["s5_log_dt"][0, dirn][:, None], (256, 64))
    pa = np.ascontiguousarray(np.stack([gp(inp["s5_A_re"][0, dirn]), gp(inp["s5_A_im"][0, dirn]), gp(ldt)], 1))

    def gb(a):
        return a.reshape(NGP, 2, 64, 16).transpose(1, 2, 0, 3).reshape(128, NGP, 16)

    def gc(a):
        return a.reshape(NGP, 2, 16, 64).transpose(1, 3, 0, 2).reshape(128, NGP, 16)
    pbc = np.ascontiguousarray(np.stack([gb(inp["s5_B_re"][0, dirn]), gb(inp["s5_B_im"][0, dirn]), gc(inp["s5_C_re"][0, dirn]), gc(inp["s5_C_im"][0, dirn])], 1))
    return pa.astype(np.float32), pbc.astype(np.float32)


def emit_s5_gen(kb, pa_d, pbc_d):
    em = kb.em
    apw = kb.sb("apw", [128, NSTEP, 3, NGP])
    bb = kb.sb("bb", [128, 2, NGP, 16])
    cT = kb.sb("cT", [128, 2, NGP, 16])
    m0 = kb.mark()
    pa = kb.sb("pa", [128, 3, NGP])
    pbc = kb.sb("pbc", [128, 4, NGP, 16])
    T = [kb.sb("g5t%d" % i, [128, NGP]) for i in range(10)]
    ki = kb.sb("g5ki", [128, NGP], I32)
    tb = [kb.sb("g5b%d" % i, [128, NGP, 16]) for i in range(2)]
    em.dma("sp", pa[:], pa_d, writes=["pa"])
    em.dma("sp", pbc[:], pbc_d, writes=["pbc"])
    K = "g5"
    Are, Aim, ldt = pa[:, 0, :], pa[:, 1, :], pa[:, 2, :]
    dt, mag, ang, angc, t, fre, fim, x1, x2, kf = [a[:] for a in T]
    ar, ai, nai = apw[:, 0, 0, :], apw[:, 0, 1, :], apw[:, 0, 2, :]

    def dve(fn):
        em.op("dve", fn, reads=["pa", K], writes=[K])

    def act(fn):
        em.op("act", fn, reads=["pa", K], writes=[K])
    act(lambda e: e.activation(out=dt, in_=ldt, func=AF.Exp))
    dve(lambda e: e.tensor_tensor(out=mag, in0=Are, in1=dt, op=ALU.mult))
    act(lambda e: e.activation(out=mag, in_=mag, func=AF.Exp))
    dve(lambda e: e.tensor_tensor(out=ang, in0=Aim, in1=dt, op=ALU.mult))
    dve(lambda e: e.tensor_copy(out=angc, in_=ang))
    emit_range_reduce_k(kb, ang, kf, ki[:], 0.0, K)
    emit_range_reduce_k(kb, angc, kf, ki[:], PI / 2, K)
    act(lambda e: e.activation(out=ang, in_=ang, func=AF.Sin))
    act(lambda e: e.activation(out=angc, in_=angc, func=AF.Sin))
    dve(lambda e: e.tensor_tensor(out=ar, in0=mag, in1=angc, op=ALU.mult))
    dve(lambda e: e.tensor_tensor(out=ai, in0=mag, in1=ang, op=ALU.mult))
    dve(lambda e: e.tensor_tensor(out=x1, in0=Are, in1=Are, op=ALU.mult))
    dve(lambda e: e.tensor_tensor(out=x2, in0=Aim, in1=Aim, op=ALU.mult))
    dve(lambda e: e.tensor_tensor(out=x1, in0=x1, in1=x2, op=ALU.add))
    dve(lambda e: e.reciprocal(out=x1, in_=x1))
    dve(lambda e: e.tensor_scalar(out=t, in0=ar, scalar1=-1.0, scalar2=None, op0=ALU.add))
    dve(lambda e: e.tensor_tensor(out=fre, in0=t, in1=Are, op=ALU.mult))
    dve(lambda e: e.tensor_tensor(out=x2, in0=ai, in1=Aim, op=ALU.mult))
    dve(lambda e: e.tensor_tensor(out=fre, in0=fre, in1=x2, op=ALU.add))
    dve(lambda e: e.tensor_tensor(out=fre, in0=fre, in1=x1, op=ALU.mult))
    dve(lambda e: e.tensor_tensor(out=fim, in0=ai, in1=Are, op=ALU.mult))
    dve(lambda e: e.tensor_tensor(out=x2, in0=t, in1=Aim, op=ALU.mult))
    dve(lambda e: e.tensor_tensor(out=fim, in0=fim, in1=x2, op=ALU.subtract))
    dve(lambda e: e.tensor_tensor(out=fim, in0=fim, in1=x1, op=ALU.mult))
    freb = fre.unsqueeze(2).to_broadcast([128, NGP, 16])
    fimb = fim.unsqueeze(2).to_broadcast([128, NGP, 16])
    Bre, Bim, Cre, Cim = pbc[:, 0], pbc[:, 1], pbc[:, 2], pbc[:, 3]

    def dvb(fn):
        em.op("dve", fn, reads=["pbc", K, "bb"], writes=["bb"])
    dvb(lambda e: e.tensor_tensor(out=bb[:, 0], in0=Bre, in1=freb, op=ALU.mult))
    dvb(lambda e: e.tensor_tensor(out=tb[0][:], in0=Bim, in1=fimb, op=ALU.mult))
    dvb(lambda e: e.tensor_tensor(out=bb[:, 0], in0=bb[:, 0], in1=tb[0][:], op=ALU.subtract))
    dvb(lambda e: e.tensor_tensor(out=bb[:, 1], in0=Bim, in1=freb, op=ALU.mult))
    dvb(lambda e: e.tensor_tensor(out=tb[1][:], in0=Bre, in1=fimb, op=ALU.mult))
    dvb(lambda e: e.tensor_tensor(out=bb[:, 1], in0=bb[:, 1], in1=tb[1][:], op=ALU.add))
    em.op("dve", lambda e: e.tensor_copy(out=cT[:, 0], in_=Cre), reads=["pbc"], writes=["cT"])
    em.op("dve", lambda e: e.tensor_scalar(out=cT[:, 1], in0=Cim, scalar1=-1.0, scalar2=None, op0=ALU.mult), reads=["pbc"], writes=["cT"])
    for j in range(NSTEP - 1):
        r0, i0 = apw[:, j, 0, :], apw[:, j, 1, :]
        r1, i1 = apw[:, j + 1, 0, :], apw[:, j + 1, 1, :]
        dve(lambda e, r0=r0: e.tensor_tensor(out=x1, in0=r0, in1=r0, op=ALU.mult))
        dve(lambda e, i0=i0: e.tensor_tensor(out=x2, in0=i0, in1=i0, op=ALU.mult))
        dve(lambda e, r1=r1: e.tensor_tensor(out=r1, in0=x1, in1=x2, op=ALU.subtract))
        dve(lambda e, r0=r0, i0=i0: e.tensor_tensor(out=x1, in0=r0, in1=i0, op=ALU.mult))
        dve(lambda e, i1=i1: e.tensor_scalar(out=i1, in0=x1, scalar1=2.0, scalar2=None, op0=ALU.mult))
    dve(lambda e: e.tensor_scalar(out=apw[:, :, 2, :], in0=apw[:, :, 1, :], scalar1=-1.0, scalar2=None, op0=ALU.mult))
    em.op("dve", lambda e: e.tensor_copy(out=T[0][:, 0:1], in_=T[0][:, 0:1]), reads=[K], writes=["apw"])
    kb.release(m0)
    return apw, bb, cT


def emit_range_reduce_k(kb, t, kf, ki, add, K):
    em = kb.em

    def dve(fn):
        em.op("dve", fn, reads=[K], writes=[K])
    if add != 0.0:
        dve(lambda e: e.tensor_scalar(out=t, in0=t, scalar1=float(add), scalar2=None, op0=ALU.add))
    dve(lambda e: e.tensor_scalar(out=kf, in0=t, scalar1=float(1 / (2 * PI)), scalar2=None, op0=ALU.mult))
    dve(lambda e: e.tensor_copy(out=ki, in_=kf))
    dve(lambda e: e.tensor_copy(out=kf, in_=ki))
    dve(lambda e: e.scalar_tensor_tensor(out=t, in0=kf, scalar=float(-2 * PI), in1=t, op0=ALU.mult, op1=ALU.add))
    dve(lambda e: e.tensor_scalar(out=kf, in0=t, scalar1=PI, scalar2=float(-2 * PI), op0=ALU.is_gt, op1=ALU.mult))
    dve(lambda e: e.tensor_tensor(out=t, in0=t, in1=kf, op=ALU.add))
    dve(lambda e: e.tensor_scalar(out=kf, in0=t, scalar1=-PI, scalar2=float(2 * PI), op0=ALU.is_lt, op1=ALU.mult))
    dve(lambda e: e.tensor_tensor(out=t, in0=t, in1=kf, op=ALU.add))


def emit_s5_scan(kb, slot, uT, apw, bb, cT, c32, dsk, hinit, finish_octet, hend_s=None, octets=range(NJ)):
    em = kb.em
    n0, n1 = (0, NT) if slot == 0 else (NCTX, NT)
    ntok = n1 - n0
    steps = []
    s = 1
    while s < ntok:
        steps.append(s)
        s *= 2
    uo = [kb.sb("s5uo%d" % i, [128, NT]) for i in range(2)]
    ub = [kb.sb("s5ub%d" % i, [128, NT], BF16) for i in range(2)]
    H = [[[kb.sb("s5H%d%d%d" % (ei, ab, ri), [128, NT]) for ri in range(2)] for ab in range(2)] for ei in range(2)]
    Hb = [kb.sb("s5Hb%d" % ei, [128, 2, NLAT], BF16) for ei in range(2)]
    bbx = [kb.sb("s5bbx%d" % i, [128, 2, 128]) for i in range(4)]
    cx = [kb.sb("s5cx%d" % i, [128, 2, 128], BF16) for i in range(4)]
    BbT = [kb.sb("s5BbT%d" % i, [128, 2, 128], BF16) for i in range(2)]
    pbt = kb.ps("s5pbt", [128, 512])
    pbu = [kb.ps("s5pbu%d" % i, [128, 512]) for i in range(2)]
    py = [kb.ps("s5py%d" % i, [128, 512]) for i in range(4)]
    for i in range(4):
        em.op("pool", lambda e, i=i: e.memset(bbx[i][:], 0.0), writes=["s5bbx%d" % i])
        em.op("pool", lambda e, i=i: e.memset(cx[i][:], 0.0), writes=["s5cx%d" % i])
    engs = ["dve", "dve"]
    for o in octets:
        ob = o % 2
        em.dma("sp", uo[ob][:], uT[o], writes=["s5uo%d" % ob])
        em.op("act", lambda e, ob=ob: e.copy(out=ub[ob][:], in_=uo[ob][:]), reads=["s5uo%d" % ob], writes=["s5ub%d" % ob])
        for pi in range(4):
            gp = 4 * o + pi
            ei = gp % 2
            eng = engs[ei]
            for gi in range(2):
                col = 16 * (2 * pi + gi)
                ps_ = slice(64 * gi, 64 * gi + 64)
                em.op("pool", lambda e, pi=pi, ps_=ps_, col=col, gp=gp: e.tensor_copy(out=bbx[pi][ps_, :, col:col + 16], in_=bb[ps_, :, gp, :]),
                      reads=["bb"], writes=["s5bbx%d" % pi])
                em.op("pool", lambda e, pi=pi, ps_=ps_, col=col, gp=gp: e.tensor_copy(out=cx[pi][ps_, :, col:col + 16], in_=cT[ps_, :, gp, :]),
                      reads=["cT"], writes=["s5cx%d" % pi])
            for ri in range(2):
                em.op("pe", lambda e, pi=pi, ri=ri: e.matmul(pbt[:, ri * 128:(ri + 1) * 128], lhsT=bbx[pi][:, ri, :], rhs=c32[:, 0, :], start=True, stop=True),
                      reads=["s5bbx%d" % pi, "consts"], writes=["P:s5pbt"])
            bt = BbT[gp % 2]
            btk = "s5BbT%d" % (gp % 2)
            em.op("act", lambda e, bt=bt: e.copy(out=bt[:].rearrange("p a b -> p (a b)"), in_=pbt[:, 0:256]), reads=["P:s5pbt"], writes=[btk])
            HA, HBf = H[ei][0], H[ei][1]
            hk = ["s5H%d%d" % (ei, 0), "s5H%d%d" % (ei, 1)]
            for (a, n) in chunks(ntok, 512):
                a += n0
                for ri in range(2):
                    em.op("pe", lambda e, ri=ri, bt=bt, a=a, n=n, ob=ob: e.matmul(pbu[ri][:, 0:n], lhsT=bt[:, ri, :], rhs=ub[ob][:, a:a + n], start=True, stop=True),
                          reads=[btk, "s5ub%d" % ob], writes=["P:s5pbu%d" % ri])
                    em.op("act", lambda e, ri=ri, a=a, n=n, HA=HA: e.copy(out=HA[ri][:, a:a + n], in_=pbu[ri][:, 0:n]), reads=["P:s5pbu%d" % ri], writes=[hk[0]])
            if slot == 1:
                last = slice(n1 - 1, n1)
                are, aim, naim = apw[:, 0, 0, gp:gp + 1], apw[:, 0, 1, gp:gp + 1], apw[:, 0, 2, gp:gp + 1]
                hre, him = hinit[:, gp, 0:1], hinit[:, gp, 1:2]
                em.op(eng, lambda e, HA=HA, last=last, hre=hre, are=are: e.scalar_tensor_tensor(out=HA[0][:, last], in0=hre, scalar=are, in1=HA[0][:, last], op0=ALU.mult, op1=ALU.add), reads=[hk[0], "apw", "hinit"], writes=[hk[0]])
                em.op(eng, lambda e, HA=HA, last=last, him=him, naim=naim: e.scalar_tensor_tensor(out=HA[0][:, last], in0=him, scalar=naim, in1=HA[0][:, last], op0=ALU.mult, op1=ALU.add), reads=[hk[0], "apw", "hinit"], writes=[hk[0]])
                em.op(eng, lambda e, HA=HA, last=last, him=him, are=are: e.scalar_tensor_tensor(out=HA[1][:, last], in0=him, scalar=are, in1=HA[1][:, last], op0=ALU.mult, op1=ALU.add), reads=[hk[0], "apw", "hinit"], writes=[hk[0]])
                em.op(eng, lambda e, HA=HA, last=last, hre=hre, aim=aim: e.scalar_tensor_tensor(out=HA[1][:, last], in0=hre, scalar=aim, in1=HA[1][:, last], op0=ALU.mult, op1=ALU.add), reads=[hk[0], "apw", "hinit"], writes=[hk[0]])
            cur = 0
            for j, s in enumerate(steps):
                S, Dd = H[ei][cur], H[ei][1 - cur]
                sk, dk = hk[cur], hk[1 - cur]
                if slot == 0:
                    oi, ii, keep = slice(n0 + s, n1), slice(n0, n1 - s), slice(n0, n0 + s)
                else:
                    oi, ii, keep = slice(n0, n1 - s), slice(n0 + s, n1), slice(n1 - s, n1)
                are, aim, naim = apw[:, j, 0, gp:gp + 1], apw[:, j, 1, gp:gp + 1], apw[:, j, 2, gp:gp + 1]
                em.op(eng, lambda e, S=S, Dd=Dd, oi=oi, ii=ii, are=are: e.scalar_tensor_tensor(out=Dd[0][:, oi], in0=S[0][:, ii], scalar=are, in1=S[0][:, oi], op0=ALU.mult, op1=ALU.add), reads=[sk, "apw"], writes=[dk])
                em.op(eng, lambda e, S=S, Dd=Dd, oi=oi, ii=ii, naim=naim: e.scalar_tensor_tensor(out=Dd[0][:, oi], in0=S[1][:, ii], scalar=naim, in1=Dd[0][:, oi], op0=ALU.mult, op1=ALU.add), reads=[sk, "apw", dk], writes=[dk])
                em.op(eng, lambda e, S=S, Dd=Dd, oi=oi, ii=ii, are=are: e.scalar_tensor_tensor(out=Dd[1][:, oi], in0=S[1][:, ii], scalar=are, in1=S[1][:, oi], op0=ALU.mult, op1=ALU.add), reads=[sk, "apw"], writes=[dk])
                em.op(eng, lambda e, S=S, Dd=Dd, oi=oi, ii=ii, aim=aim: e.scalar_tensor_tensor(out=Dd[1][:, oi], in0=S[0][:, ii], scalar=aim, in1=Dd[1][:, oi], op0=ALU.mult, op1=ALU.add), reads=[sk, "apw", dk], writes=[dk])
                for ri in range(2):
                    em.op("act", lambda e, S=S, Dd=Dd, keep=keep, ri=ri: e.copy(out=Dd[ri][:, keep], in_=S[ri][:, keep]), reads=[sk], writes=[dk])
                cur = 1 - cur
            R = H[ei][cur]
            rk = hk[cur]
            if hend_s is not None:
                for ri in range(2):
                    em.op("act", lambda e, R=R, ri=ri, gp=gp: e.copy(out=hend_s[:, gp, ri:ri + 1], in_=R[ri][:, n1 - 1:n1]), reads=[rk], writes=["hend"])
            hbk = "s5Hb%d" % ei
            for ri in range(2):
                em.op("act", lambda e, R=R, ri=ri, ei=ei: e.copy(out=Hb[ei][:, ri, :], in_=R[ri][:, NCTX:NT]), reads=[rk], writes=[hbk])
            for ci in range(4):
                for ri in range(2):
                    first = (pi == 0 and ri == 0)
                    lastm = (pi == 3 and ri == 1)
                    em.op("pe", lambda e, ci=ci, pi=pi, ri=ri, ei=ei, first=first, lastm=lastm: e.matmul(py[ci][:, :], lhsT=cx[pi][:, ri, :], rhs=Hb[ei][:, ri, ci * 512:(ci + 1) * 512], start=first, stop=lastm),
                          reads=["s5cx%d" % pi, hbk], writes=["P:s5py%d" % ci])
        finish_octet(o, py, uo[ob], "s5uo%d" % ob)


def build_E1(octets=range(NJ)):
    kb = KB()
    em = kb.em
    uT = kb.din("uT", [NJ, 128, NT])
    pa_d = kb.din("pa", [128, 3, NGP])
    pbc_d = kb.din("pbc", [128, 4, NGP, 16])
    dskd = kb.din("dsk", [128, NJ])
    cst = kb.din("cst", [128, 4, 128])
    y0T = kb.dout("y0T", [NJ, 128, NLAT])
    hend = kb.dout("hend", [128, NGP, 2])
    c32, c16, epsb = emit_consts(kb, cst)
    dsk = kb.sb("dsk", [128, NJ])
    em.dma("sp", dsk[:], dskd, writes=["dsk"])
    apw, bb, cT = emit_s5_gen(kb, pa_d, pbc_d)
    hend_s = kb.sb("hend_s", [128, NGP, 2])
    em.op("pool", lambda e: e.memset(hend_s[:], 0.0), writes=["hend"])
    ystg = [kb.sb("ystg%d" % i, [128, NLAT]) for i in range(2)]

    def fin(o, py, uo, uok):
        yi = o % 2
        for ci in range(4):
            em.op("dve", lambda e, ci=ci, yi=yi, uo=uo, o=o: e.scalar_tensor_tensor(out=ystg[yi][:, ci * 512:(ci + 1) * 512], in0=uo[:, NCTX + ci * 512:NCTX + (ci + 1) * 512], scalar=dsk[:, o:o + 1], in1=py[ci][:, :], op0=ALU.mult, op1=ALU.add),
                  reads=["P:s5py%d" % ci, uok, "dsk"], writes=["ystg%d" % yi])
        kb.store("sp", y0T[o], ystg[yi][:], reads=["ystg%d" % yi])
    emit_s5_scan(kb, 0, uT, apw, bb, cT, c32, dsk, None, fin, hend_s, octets)
    kb.store("sp", hend, hend_s[:], reads=["hend"])
    return kb.close()


def run_E1(inp, resD, octets=range(NJ), cores=range(NCORE)):
    cst = make_consts()
    maps = []
    for r in cores:
        pa, pbc = s5_params(inp, r % 2)
        maps.append({"uT": resD[r]["uT"], "pa": pa, "pbc": pbc, "dsk": vec_pj(inp["s5_D"][0]), "cst": cst})
    res = run_bass_kernel_spmd(build_E1(octets), maps, core_ids=list(range(len(maps))))
    return res.results


def emit_proj_fm(kb, actT, akey, ncols, col_off, wd, evac, tagp):
    em = kb.em
    nh = ncols // 2
    tbs = chunks(nh, 512)
    m0 = kb.mark()
    aT = kb.sb(tagp + "_aT", [128, NJ, nh], BF16)
    wst = [kb.sb(tagp + "_wst%d" % i, [128, 8, BW]) for i in range(2)]
    wb = [kb.sb(tagp + "_wb%d" % i, [128, NJ, BW], BF16) for i in range(2)]
    pacc = [kb.ps(tagp + "_pacc%d" % i, [128, 512]) for i in range(3)]
    st = {"n": 0, "s": 0}
    na = 0
    for half in range(2):
        c0 = half * nh
        for j in range(NJ):
            em.dma("sp", aT[:, j, :], actT[j, :, col_off + c0:col_off + c0 + nh], reads=[(akey, j)], writes=[tagp + "_aT"])
        for blk in range(D // BW):
            bi = st["n"] % 2
            st["n"] += 1
            for pc in range(4):
                si = st["s"] % 2
                st["s"] += 1
                em.dma("sp", wst[si][:], wd[blk, pc], writes=[tagp + "_wst%d" % si])
                em.op("pool", lambda e, si=si, bi=bi, pc=pc: e.tensor_copy(out=wb[bi][:, pc * 8:(pc + 1) * 8, :], in_=wst[si][:]),
                      reads=[tagp + "_wst%d" % si], writes=[tagp + "_wb%d" % bi])
            wk = tagp + "_wb%d" % bi
            for sub in range(BW // 128):
                fb = blk * (BW // 128) + sub
                for (ta, tn) in tbs:
                    p = pacc[na % 3]
                    pk = "P:" + tagp + "_pacc%d" % (na % 3)
                    na += 1
                    for j in range(NJ):
                        em.op("pe", lambda e, p=p, j=j, bi=bi, sub=sub, ta=ta, tn=tn: e.matmul(p[:, 0:tn], lhsT=wb[bi][:, j, sub * 128:(sub + 1) * 128], rhs=aT[:, j, ta:ta + tn], start=(j == 0), stop=(j == NJ - 1)),
                              reads=[wk, tagp + "_aT"], writes=[pk])
                    evac(fb, c0, ta, tn, p, pk, aT)
    kb.release(m0)


def build_E2(octets=range(NJ), phases=(1, 2, 3)):
    kb = KB()
    em = kb.em
    nc = kb.nc
    uT = kb.din("uT", [NJ, 128, NT])
    pa_d = kb.din("pa", [128, 3, NGP])
    pbc_d = kb.din("pbc", [128, 4, NGP, 16])
    hin_d = kb.din("hinit", [128, NGP, 2])
    y0T = kb.din("y0T", [NJ, 128, NLAT])
    szT = kb.din("szT", [NJ, 128, NT], BF16)
    wglu = kb.din("wglu", [D // BW, 4, 128, 8, BW])
    wout = kb.din("wout", [D // BW, 4, 128, 8, BW])
    x1T = kb.din("x1T", [NJ, 128, NT])
    vec2 = kb.din("vec2", [128, 3, NJ])
    cst = kb.din("cst", [128, 4, 128])
    outT = kb.dout("outT", [NJ, 128, NLAT])
    ygT = nc.dram_tensor("ygT_e", [NJ, 128, NLAT], BF16).ap()
    y3T = nc.dram_tensor("y3T_e", [NJ, 128, NLAT], BF16).ap()
    c32, c16, epsb = emit_consts(kb, cst)
    gn = emit_gn(kb, vec2)
    if 1 in phases:
        m0 = kb.mark()
        hinit = kb.sb("hinit", [128, NGP, 2])
        em.dma("sp", hinit[:], hin_d, writes=["hinit"])
        apw, bb, cT = emit_s5_gen(kb, pa_d, pbc_d)
        y0s = [kb.sb("y0s%d" % i, [128, NLAT]) for i in range(2)]
        ygs = [kb.sb("ygs%d" % i, [128, NLAT], BF16) for i in range(2)]

        def fin(o, py, uo, uok):
            yi = o % 2
            em.dma("sp", y0s[yi][:], y0T[o], writes=["y0s%d" % yi])
            for ci in range(4):
                em.op("dve", lambda e, ci=ci, yi=yi: e.tensor_tensor(out=y0s[yi][:, ci * 512:(ci + 1) * 512], in0=py[ci][:, :], in1=y0s[yi][:, ci * 512:(ci + 1) * 512], op=ALU.add),
                      reads=["P:s5py%d" % ci, "y0s%d" % yi], writes=["y0s%d" % yi])
            em.op("act", lambda e, yi=yi: e.activation(out=ygs[yi][:], in_=y0s[yi][:], func=AF.Gelu), reads=["y0s%d" % yi], writes=["ygs%d" % yi])
            em.dma("sp", ygT[o], ygs[yi][:], reads=["ygs%d" % yi], writes=[("ygT", o)])
        emit_s5_scan(kb, 1, uT, apw, bb, cT, c32, None, hinit, fin, None, octets)
        kb.release(m0)
    if 2 in phases:
        m1 = kb.mark()
        szs = [kb.sb("szs%d" % i, [128, 512], BF16) for i in range(2)]
        sgs = [kb.sb("sgs%d" % i, [128, 512]) for i in range(2)]
        y3s = [kb.sb("y3s%d" % i, [128, 512], BF16) for i in range(2)]
        cnt = {"n": 0}

        def evac(fb, c0, ta, tn, p, pk, aT):
            i = cnt["n"] % 2
            cnt["n"] += 1
            em.dma("act", szs[i][:, 0:tn], szT[fb, :, NCTX + c0 + ta:NCTX + c0 + ta + tn], writes=["szs%d" % i])
            em.op("act", lambda e: e.activation(out=sgs[i][:, 0:tn], in_=p[:, 0:tn], func=AF.Sigmoid), reads=[pk], writes=["sgs%d" % i])
            em.op("dve", lambda e: e.tensor_tensor(out=sgs[i][:, 0:tn], in0=sgs[i][:, 0:tn], in1=aT[:, fb, ta:ta + tn], op=ALU.mult), reads=["sgs%d" % i, "gl_aT"], writes=["sgs%d" % i])
            em.op("dve", lambda e: e.tensor_tensor(out=y3s[i][:, 0:tn], in0=sgs[i][:, 0:tn], in1=szs[i][:, 0:tn], op=ALU.mult), reads=["sgs%d" % i, "szs%d" % i], writes=["y3s%d" % i])
            em.dma("act", y3T[fb, :, c0 + ta:c0 + ta + tn], y3s[i][:, 0:tn], reads=["y3s%d" % i], writes=[("actT", fb)])
        emit_proj_fm(kb, ygT, "ygT", NLAT, 0, wglu, evac, "gl")
        kb.release(m1)
    if 3 in phases:
        emit_outproj_residual(kb, y3T, wout, x1T, gn, outT, c32, epsb, NLAT, NCTX, "e")
    return kb.close()


def run_E2(inp, mod, x1, resD, resE1):
    cst = make_consts()
    wglu = tile_w(inp["s5_w_glu"][0])
    wout = tile_w(inp["s5_w_out"][0])
    maps = []
    for r in range(NCORE):
        pa, pbc = s5_params(inp, 1 - (r % 2))
        maps.append({"uT": resD[r]["uT"], "pa": pa, "pbc": pbc, "hinit": resE1[r ^ 1]["hend"], "y0T": resE1[r]["y0T"], "szT": resD[r]["szT"],
                     "wglu": wglu, "wout": wout, "x1T": x1[r], "vec2": mod_vec2(mod, 1, r // 2, inp["norm_post"][1]), "cst": cst})
    res = run_bass_kernel_spmd(build_E2(), maps, core_ids=list(range(NCORE)))
    return res.results


def kernel(**inputs):
    inp = {k: np.asarray(v) for k, v in inputs.items()}
    mod = run_A(inp)
    lay = core_layout(inp)
    resB = run_B(inp, mod, lay)
    resC = run_C(inp, mod, lay, resB)
    x1 = [resC[r]["x1T"] for r in range(NCORE)]
    del resB, resC
    resD = run_D(inp, mod, x1)
    resE1 = run_E1(inp, resD)
    resE2 = run_E2(inp, mod, x1, resD, resE1)
    out = np.empty((4, 2 * NLAT, D), np.float32)
    for r in range(NCORE):
        b, hf = r // 2, r % 2
        o = np.asarray(resE2[r]["outT"]).reshape(D, NLAT).T
        if hf == 1:
            o = o[::-1]
        out[b, hf * NLAT:(hf + 1) * NLAT] = o
    return out
```
